# Optimizing a Trainium2 kernel written in Bass

```python
import jax, jax.numpy as jnp
from jax import lax
import numpy as np

D_MODEL = 1024
BATCH = 2
SEQ = 8192
DEPTH = 2

MIX_WIDTH = D_MODEL
HEAD_DIM = 64
ATTN_HEADS = 8
ATTN_KV_HEADS = 2
ATTN_REP = ATTN_HEADS // ATTN_KV_HEADS
ATTN_WIDTH = ATTN_HEADS * HEAD_DIM
KV_WIDTH = ATTN_KV_HEADS * HEAD_DIM
WINDOW = 128
BLOCK = 128
ATTN_SCALE = HEAD_DIM ** -0.5
GMLP_HEADS = 4
GMLP_WIDTH = GMLP_HEADS * HEAD_DIM
CHUNK = 128
POOL_GROUPS = 4
POOL_GROUP_DIM = 64
POOL_WIDTH = POOL_GROUPS * POOL_GROUP_DIM
POOL_WINDOWS = (2, 4, 8, 16)
IN_WIDTH = ATTN_WIDTH + 2 * KV_WIDTH + 2 * GMLP_WIDTH + POOL_WIDTH
D_FF = 2816
EPS = 1e-6

kernel_name = "hybrid_macaron_swa_gmlp_pool"


def rms_norm(x, g):
    xf = x.astype(jnp.float32)
    y = xf * lax.rsqrt(jnp.mean(xf * xf, axis=-1, keepdims=True) + EPS)
    return (y * g.astype(jnp.float32)).astype(x.dtype)


def swiglu(h, w_gate, w_up, w_down):
    return (jax.nn.silu(h @ w_gate) * (h @ w_up)) @ w_down


def sliding_window_attention(q, k, v, sinks):
    b, s = q.shape[0], q.shape[1]
    nb = s // BLOCK
    qb = q.reshape(b, nb, BLOCK, ATTN_KV_HEADS, ATTN_REP, HEAD_DIM)
    pad = ((0, 0), (BLOCK, 0), (0, 0), (0, 0))
    kp = jnp.pad(k, pad).reshape(b, nb + 1, BLOCK, ATTN_KV_HEADS, HEAD_DIM)
    vp = jnp.pad(v, pad).reshape(b, nb + 1, BLOCK, ATTN_KV_HEADS, HEAD_DIM)
    kb = jnp.concatenate([kp[:, :-1], kp[:, 1:]], axis=2)
    vb = jnp.concatenate([vp[:, :-1], vp[:, 1:]], axis=2)
    scores = jnp.einsum('bnqgrd,bnkgd->bngrqk', qb, kb).astype(jnp.float32) * ATTN_SCALE
    qi = jnp.arange(BLOCK)[:, None]
    kj = jnp.arange(2 * BLOCK)[None, :]
    rel = qi + BLOCK - kj
    band = (rel >= 0) & (rel < WINDOW)
    not_pad = (jnp.arange(nb)[:, None, None] > 0) | (kj >= BLOCK)[None]
    valid = band[None] & not_pad
    scores = jnp.where(valid[None, :, None, None], scores, -jnp.inf)
    sink = sinks.astype(jnp.float32).reshape(ATTN_KV_HEADS, ATTN_REP)[None, None, :, :, None, None]
    m = jnp.maximum(jnp.max(scores, axis=-1, keepdims=True), sink)
    p = jnp.exp(scores - m)
    p = p / (jnp.sum(p, axis=-1, keepdims=True) + jnp.exp(sink - m))
    o = jnp.einsum('bngrqk,bnkgd->bnqgrd', p.astype(v.dtype), vb)
    return o.reshape(b, s, ATTN_WIDTH)


def chunked_spatial_gating(u, v, v_norm, w_s, bias):
    b, s = u.shape[0], u.shape[1]
    nc = s // CHUNK
    v = rms_norm(v, v_norm)
    vc = v.reshape(b, nc, CHUNK, GMLP_HEADS, HEAD_DIM)
    causal = jnp.tril(jnp.ones((CHUNK, CHUNK), dtype=bool))
    w = jnp.where(causal[None], w_s, jnp.zeros((), w_s.dtype))
    f = jnp.einsum('hij,bcjhd->bcihd', w, vc) + bias.T[None, None, :, :, None]
    return u * f.reshape(b, s, GMLP_WIDTH)


def multiscale_pool(p_in, pool_w, pool_scale):
    s = p_in.shape[1]
    pf = p_in.astype(jnp.float32)
    cs = jnp.pad(jnp.cumsum(pf, axis=1), ((0, 0), (1, 0), (0, 0)))
    pos1 = jnp.arange(1, s + 1, dtype=jnp.float32)
    outs = []
    for g, w in enumerate(POOL_WINDOWS):
        sl = slice(g * POOL_GROUP_DIM, (g + 1) * POOL_GROUP_DIM)
        csg = cs[..., sl]
        lagged = jnp.pad(csg, ((0, 0), (w, 0), (0, 0)))[:, 1:s + 1]
        count = jnp.minimum(pos1, float(w))[None, :, None]
        diff = (csg[:, 1:] - lagged) / count - pf[..., sl]
        outs.append(diff.astype(p_in.dtype) @ pool_w[g])
    return jnp.concatenate(outs, axis=-1) * pool_scale


def setup_inputs(seed: int = 0) -> dict:
    key = jax.random.key(seed)
    ks = jax.random.split(key, 24)
    f32 = jnp.float32

    def nrm(k, shape, scale):
        return jax.random.normal(k, shape, f32) * scale

    def gain(k, shape):
        return 1.0 + 0.02 * jax.random.normal(k, shape, f32)

    return {
        "x": jax.random.normal(ks[0], (BATCH, SEQ, D_MODEL), f32),
        "ffn1_norm": gain(ks[1], (DEPTH, D_MODEL)),
        "ffn1_w_gate": nrm(ks[2], (DEPTH, D_MODEL, D_FF), D_MODEL ** -0.5),
        "ffn1_w_up": nrm(ks[3], (DEPTH, D_MODEL, D_FF), D_MODEL ** -0.5),
        "ffn1_w_down": nrm(ks[4], (DEPTH, D_FF, D_MODEL), D_FF ** -0.5),
        "mix_norm": gain(ks[5], (DEPTH, D_MODEL)),
        "w_in": nrm(ks[6], (DEPTH, D_MODEL, IN_WIDTH), D_MODEL ** -0.5),
        "attn_sinks": nrm(ks[7], (DEPTH, ATTN_HEADS), 0.5),
        "gmlp_v_norm": gain(ks[8], (DEPTH, GMLP_WIDTH)),
        "gmlp_w_s": nrm(ks[9], (DEPTH, GMLP_HEADS, CHUNK, CHUNK), CHUNK ** -0.5),
        "gmlp_b": gain(ks[10], (DEPTH, GMLP_HEADS, CHUNK)),
        "pool_w": nrm(ks[11], (DEPTH, POOL_GROUPS, POOL_GROUP_DIM, POOL_GROUP_DIM), POOL_GROUP_DIM ** -0.5),
        "pool_scale": gain(ks[12], (DEPTH, POOL_WIDTH)),
        "w_out": nrm(ks[13], (DEPTH, MIX_WIDTH, D_MODEL), MIX_WIDTH ** -0.5),
        "ffn2_norm": gain(ks[14], (DEPTH, D_MODEL)),
        "ffn2_w_gate": nrm(ks[15], (DEPTH, D_MODEL, D_FF), D_MODEL ** -0.5),
        "ffn2_w_up": nrm(ks[16], (DEPTH, D_MODEL, D_FF), D_MODEL ** -0.5),
        "ffn2_w_down": nrm(ks[17], (DEPTH, D_FF, D_MODEL), D_FF ** -0.5),
        "final_norm": gain(ks[18], (D_MODEL,)),
    }


def reference(x, ffn1_norm, ffn1_w_gate, ffn1_w_up, ffn1_w_down, mix_norm, w_in,
              attn_sinks, gmlp_v_norm, gmlp_w_s, gmlp_b, pool_w, pool_scale, w_out,
              ffn2_norm, ffn2_w_gate, ffn2_w_up, ffn2_w_down, final_norm):
    b, s = x.shape[0], x.shape[1]
    o_k = ATTN_WIDTH
    o_v = o_k + KV_WIDTH
    o_u = o_v + KV_WIDTH
    o_g = o_u + GMLP_WIDTH
    o_p = o_g + GMLP_WIDTH
    for l in range(DEPTH):
        h = rms_norm(x, ffn1_norm[l])
        x = x + 0.5 * swiglu(h, ffn1_w_gate[l], ffn1_w_up[l], ffn1_w_down[l])
        h = rms_norm(x, mix_norm[l])
        z = h @ w_in[l]
        q = z[..., :o_k].reshape(b, s, ATTN_HEADS, HEAD_DIM)
        k = z[..., o_k:o_v].reshape(b, s, ATTN_KV_HEADS, HEAD_DIM)
        v = z[..., o_v:o_u].reshape(b, s, ATTN_KV_HEADS, HEAD_DIM)
        g_u = jax.nn.gelu(z[..., o_u:o_g])
        g_v = jax.nn.gelu(z[..., o_g:o_p])
        p_in = z[..., o_p:]
        y_attn = sliding_window_attention(q, k, v, attn_sinks[l])
        y_gmlp = chunked_spatial_gating(g_u, g_v, gmlp_v_norm[l], gmlp_w_s[l], gmlp_b[l])
        y_pool = multiscale_pool(p_in, pool_w[l], pool_scale[l])
        y = jnp.concatenate([y_attn, y_gmlp, y_pool], axis=-1)
        x = x + y @ w_out[l]
        h = rms_norm(x, ffn2_norm[l])
        x = x + 0.5 * swiglu(h, ffn2_w_gate[l], ffn2_w_up[l], ffn2_w_down[l])
    return rms_norm(x, final_norm)
```

```python
import contextlib
import numpy as np
import concourse.bass as bass
import concourse.mybir as mybir
from concourse.bass_utils import run_bass_kernel_spmd

F32 = mybir.dt.float32
BF16 = mybir.dt.bfloat16
AF = mybir.ActivationFunctionType
ALU = mybir.AluOpType
AX = mybir.AxisListType

ENGS = ("pe", "act", "dve", "pool", "sp")

D = 1024
DFF = 2816
NFC = DFF // 128
DEPTH = 2
SEQ = 8192
BATCH = 2
NCORES = 8
OWN = 2048
HALO = 256
T = OWN + HALO
NB = T // 128
OWNB = HALO // 128
INW = 1536
EPS = 1e-6
GROUPS = [4, 4, 4, 4, 3, 3]
NCV = 78
HEAD_PERM = [0, 4, 1, 5, 2, 6, 3, 7]
MASKV = -30000.0


class Res:
    __slots__ = ("name", "writer", "readers")

    def __init__(self, name=""):
        self.name = name
        self.writer = None
        self.readers = []


class Prog:
    def __init__(self, nc):
        self.nc = nc
        self.ops = {e: [] for e in ENGS}
        self.count = {e: 0 for e in ENGS}
        self.waited = {e: {} for e in ENGS}
        self.dma_waited = {e: {} for e in ENGS}
        self.pending = {e: {} for e in ENGS}
        self.n_dma_sem = 0
        self.dma_count = {}

    def _collect(self, reads, writes):
        deps = []
        for r in reads:
            if r.writer is not None:
                deps.append(r.writer)
        for w in writes:
            if w.writer is not None:
                deps.append(w.writer)
            deps.extend(w.readers)
        return deps

    def _waits(self, eng, deps):
        waits = {}
        if self.pending[eng]:
            for de, idx in self.pending[eng].items():
                deps = deps + [(de, idx)]
            self.pending[eng] = {}
        for d in deps:
            if d[0] == "dma":
                _, si, val = d
                if self.dma_waited[eng].get(si, 0) >= val:
                    continue
                key = ("dma", si)
                waits[key] = max(waits.get(key, 0), val)
            else:
                de, idx = d
                if idx <= 0:
                    continue
                if de == eng and eng in ("pe", "sp"):
                    continue
                if self.waited[eng].get(de, 0) >= idx:
                    continue
                waits[de] = max(waits.get(de, 0), idx)
        out = []
        for k, v in waits.items():
            if isinstance(k, tuple):
                self.dma_waited[eng][k[1]] = v
            else:
                self.waited[eng][k] = v
            out.append((k, v))
        return out

    def barrier(self):
        snap = dict(self.count)
        for e in ENGS:
            for de, idx in snap.items():
                if de == "sp":
                    continue
                self.pending[e][de] = max(self.pending[e].get(de, 0), idx)

    def task(self, eng, instrs, reads=(), writes=()):
        psr = [r for r in reads if r.name.startswith("ps")]
        if psr:
            reads = [r for r in reads if not r.name.startswith("ps")]
            writes = list(writes) + psr
        deps = self._collect(reads, writes)
        waits = self._waits(eng, deps)
        self.count[eng] += 1
        me = (eng, self.count[eng])
        self.ops[eng].append((waits, list(instrs), None))
        for r in reads:
            r.readers.append(me)
        for w in writes:
            w.writer = me
            w.readers = []
        return me

    def new_dma_sem(self):
        i = self.n_dma_sem
        self.n_dma_sem += 1
        self.dma_count[i] = 0
        return i

    def dma(self, eng, instrs, reads=(), writes=(), sem=None):
        deps = self._collect(reads, writes)
        waits = self._waits(eng, deps)
        if sem is None:
            sem = self.new_dma_sem()
        self.dma_count[sem] += 16 * len(instrs)
        me = ("dma", sem, self.dma_count[sem])
        self.ops[eng].append((waits, list(instrs), sem))
        for r in reads:
            r.readers.append(me)
        for w in writes:
            w.writer = me
            w.readers = []
        return me

    def emit(self, final_waits=()):
        nc = self.nc
        with contextlib.ExitStack() as st:
            esem = {e: st.enter_context(nc.semaphore("s_" + e)) for e in ENGS if e != "sp"}
            dsem = [st.enter_context(nc.semaphore("d%d" % i)) for i in range(self.n_dma_sem)]
            block = st.enter_context(nc.Block())
            ops = self.ops

            def run(ename, e):
                for waits, instrs, dsi in ops[ename]:
                    for k, v in waits:
                        if isinstance(k, tuple):
                            e.wait_ge(dsem[k[1]], v)
                        else:
                            e.wait_ge(esem[k], v)
                    n = len(instrs)
                    for j, f in enumerate(instrs):
                        ins = f(e)
                        if dsi is not None:
                            ins.then_inc(dsem[dsi], 16)
                        elif j == n - 1:
                            ins.then_inc(esem[ename], 1)

            @block.tensor
            def _(e):
                run("pe", e)

            @block.scalar
            def _(e):
                run("act", e)

            @block.vector
            def _(e):
                run("dve", e)

            @block.gpsimd
            def _(e):
                run("pool", e)

            @block.sync
            def _(e):
                run("sp", e)
                for d in final_waits:
                    if d[0] == "dma":
                        e.wait_ge(dsem[d[1]], d[2])
                    else:
                        e.wait_ge(esem[d[0]], d[1])


def I(name, *args, **kw):
    return lambda e: getattr(e, name)(*args, **kw)


def split_tiles(b0, b1, mx=4):
    n = b1 - b0
    k = -(-n // mx)
    base, rem = divmod(n, k)
    out = []
    b = b0
    for i in range(k):
        s = base + (1 if i < rem else 0)
        out.append((b, b + s))
        b += s
    return out


def build_program(stop_after=None, parts=("z", "tm", "attn", "gmlp", "pool", "wout")):
    nc = bass.Bass("TRN2", target_bir_lowering=False)

    def din(name, shape):
        return nc.dram_tensor(name, list(shape), F32, kind="ExternalInput").ap()

    xT_d = din("xT", [D, T])
    wg_d = din("wg", [2 * DEPTH, D, DFF])
    wu_d = din("wu", [2 * DEPTH, D, DFF])
    wd_d = din("wd", [2 * DEPTH, DFF, D])
    win_d = din("win", [DEPTH, D, INW])
    wout_d = din("wout", [DEPTH, D, D])
    cvec_d = din("cvec", [128, NCV])
    poolc_d = din("poolc", [128, 32])
    vng_d = din("vng", [128, DEPTH * 256])
    wsT_d = din("wsT", [128, DEPTH * 4 * 128])
    tril_d = din("trilT", [128, 128])
    biasB_d = din("biasB", [128, DEPTH * 256])
    poolbd_d = din("poolbd", [128, DEPTH * 256])
    masks_d = din("masks", [128, 512])
    ident_d = din("ident", [128, 128])
    ones_d = din("onesm", [128, 128])
    out_d = nc.dram_tensor("out", [D, OWN], F32, kind="ExternalOutput").ap()

    with contextlib.ExitStack() as st:
        def sb(name, shape, dt):
            return st.enter_context(nc.sbuf_tensor("s_" + name, list(shape), dt))

        X = sb("X", [128, 8, T], F32)
        WR = sb("WR", [128, 24576], BF16)
        PHN = 16016
        PH = sb("PH", [128, PHN], F32)
        cvec = sb("cvec", [128, NCV], F32)
        poolc = sb("poolc", [128, 32], F32)
        vng = sb("vng", [128, DEPTH * 256], F32)
        WT = sb("WT", [128, DEPTH * 512], BF16)
        tril = sb("tril", [128, 128], BF16)
        biasB = sb("biasB", [128, DEPTH * 256], F32)
        BD = sb("BD", [128, DEPTH * 256], BF16)
        masks = sb("masks", [128, 512], F32)
        ident = sb("ident", [128, 128], BF16)
        onesm = sb("onesm", [128, 128], BF16)
        sqb = sb("sqb", [128, 2, 512], BF16)
        sdb = sb("sdb", [128, 512], F32)
        Rb = sb("Rb", [128, 512], F32)
        small = sb("small", [128, 64], F32)
        PS = st.enter_context(nc.psum_tensor("PS", [128, 8, 512], F32))

        P = Prog(nc)
        rPS = [Res("ps%d" % i) for i in range(8)]
        rX = [[Res("x%d_%d" % (b, k)) for k in range(8)] for b in range(NB)]
        rW = [Res("w0"), Res("w1")]
        wsem = [P.new_dma_sem(), P.new_dma_sem()]
        wctr = [0]
        rsq = [Res("sq0"), Res("sq1")]
        rsd = Res("sd")
        rR = Res("R")
        rsmall = Res("small")

        def ph_f32(off, n):
            return PH[:, off:off + n]

        def ph_bf(off, n):
            return PH[:, off:off + n // 2].bitcast(BF16)

        Hall = ph_bf(0, 8 * T).rearrange("p (k t) -> p k t", k=8)
        aT = [ph_bf(9216 + i * 1024, 2048).rearrange("p (j t) -> p j t", j=4) for i in range(2)]
        stmp = [ph_f32(11264 + i * 512, 512) for i in range(2)]
        rH = [[Res("h%d_%d" % (b, k)) for k in range(8)] for b in range(NB)]
        raT = [[Res("aT%d_%d" % (i, j)) for j in range(4)] for i in range(2)]
        rst = [Res("st0"), Res("st1")]
        o = 0
        Hm = ph_bf(o, 4096).rearrange("p (k t) -> p k t", k=8); o += 2048
        QTt = ph_bf(o, 2048).rearrange("p (k t) -> p k t", k=4); o += 1024
        GUTt = ph_bf(o, 1024).rearrange("p (k t) -> p k t", k=2); o += 512
        GVNt = ph_bf(o, 1024).rearrange("p (b c) -> p b c", b=4); o += 512
        YTt = ph_bf(o, 4096).rearrange("p (k t) -> p k t", k=8); o += 2048
        dTt = ph_bf(o, 1056).rearrange("p (k t) -> p k t", k=2); o += 528
        Pm = ph_bf(o, 1024).rearrange("p (s k) -> p s k", s=4); o += 512
        PTs = ph_bf(o, 1024).rearrange("p (s k q) -> p s k q", s=4, k=2); o += 512
        KT = ph_bf(o, T); o += T // 2
        VTM = ph_bf(o, T).rearrange("p (b c) -> p b c", b=NB); o += T // 2
        PTt = ph_f32(o, 1056).rearrange("p (k t) -> p k t", k=2); o += 1056
        SM = [ph_f32(o + i * 1040, 1040).rearrange("p (s k) -> p s k", s=4) for i in range(2)]; o += 2080
        Eb = ph_f32(o, 1040).rearrange("p (s k) -> p s k", s=4); o += 1040
        Sa = ph_f32(o, 528); o += 528
        Sb = ph_f32(o, 528); o += 528
        glb = ph_f32(o, 256); o += 256
        gsq = ph_f32(o, 256); o += 256
        tgb = ph_f32(o, 256); o += 256
        t16 = ph_f32(o, 16); o += 16
        assert o <= PHN, o
        rHm, rQT, rGUT, rdT, rPm, rPTs, rPTt, rE = (Res(n) for n in "Hm QT GUT dT Pm PTs PTt E".split())
        rGVN = [Res("gvn%d" % i) for i in range(4)]
        rYT = [Res("yt%d" % i) for i in range(4)]
        rKT = [Res("kt%d" % i) for i in range(NB)]
        rVT = [Res("vt%d" % i) for i in range(NB)]
        rSM = [Res("sm0"), Res("sm1")]
        rSa, rSb, rgl, rgsq, rtg, rt16 = (Res(n) for n in "Sa Sb gl gsq tg t16".split())

        PSb = [PS[:, i, :] for i in range(8)]

        rc = Res("consts")
        cdeps = []
        cdeps.append(P.dma("sp", [
            I("dma_start", out=cvec[:], in_=cvec_d),
            I("dma_start", out=poolc[:], in_=poolc_d),
            I("dma_start", out=vng[:], in_=vng_d),
            I("dma_start", out=biasB[:], in_=biasB_d),
            I("dma_start", out=masks[:], in_=masks_d),
        ], writes=[rc]))
        rcb = Res("constsb")
        P.dma("pool", [
            I("dma_start", out=WT[:], in_=wsT_d),
            I("dma_start", out=tril[:], in_=tril_d),
            I("dma_start", out=BD[:], in_=poolbd_d),
            I("dma_start", out=ident[:], in_=ident_d),
            I("dma_start", out=onesm[:], in_=ones_d),
        ], writes=[rcb])
        xTv = xT_d.rearrange("(k p) t -> p k t", p=128)
        for (b0, b1) in split_tiles(0, NB):
            P.dma("sp", [I("dma_start", out=X[:, :, b0 * 128:b1 * 128], in_=xTv[:, :, b0 * 128:b1 * 128])],
                  writes=[rX[b][k] for b in range(b0, b1) for k in range(8)])
        P.task("dve", [I("tensor_tensor", out=WT[:].rearrange("p (a i) -> p a i", i=128),
                         in0=WT[:].rearrange("p (a i) -> p a i", i=128),
                         in1=tril[:].unsqueeze(1).broadcast_to([128, DEPTH * 4, 128]), op=ALU.mult)],
               reads=[rc], writes=[rcb])

        def cv(col, n=1):
            return cvec[:, col:col + n]

        def next_slot():
            s = wctr[0] % 2
            wctr[0] += 1
            return s

        loaded = {}

        def load_group(fid, g):
            if (fid, g) in loaded:
                return loaded[(fid, g)]
            s = next_slot()
            c0 = sum(GROUPS[:g])
            G = GROUPS[g]
            base = s * 12288
            wgv = WR[:, base:base + 4096].rearrange("p (k n) -> p k n", k=8)[:, :, 0:G * 128]
            wuv = WR[:, base + 4096:base + 8192].rearrange("p (k n) -> p k n", k=8)[:, :, 0:G * 128]
            wdv = WR[:, base + 8192:base + 12288].rearrange("p (j n) -> p j n", j=4)[:, 0:G, :]
            P.dma("pool", [
                I("dma_start", out=wgv, in_=wg_d[fid].rearrange("(k p) n -> p k n", p=128)[:, :, c0 * 128:(c0 + G) * 128]),
                I("dma_start", out=wuv, in_=wu_d[fid].rearrange("(k p) n -> p k n", p=128)[:, :, c0 * 128:(c0 + G) * 128]),
                I("dma_start", out=wdv, in_=wd_d[fid][c0 * 128:(c0 + G) * 128, :].rearrange("(j p) n -> p j n", p=128)),
            ], writes=[rW[s]], sem=wsem[s])
            loaded[(fid, g)] = s
            return s

        sqc = [0]

        def rmsnorm_tile(b0, b1, gcol, out_ap_fn, out_res_fn):
            t0, t1 = b0 * 128, b1 * 128
            NT = t1 - t0
            for k in range(8):
                i = sqc[0] % 2
                sqc[0] += 1
                P.task("pool", [I("tensor_tensor", out=sqb[:, i, 0:NT], in0=X[:, k, t0:t1], in1=X[:, k, t0:t1], op=ALU.mult)],
                       reads=[rX[b][k] for b in range(b0, b1)], writes=[rsq[i]])
                P.task("pe", [I("matmul", PSb[7][:, 0:NT], lhsT=onesm[:], rhs=sqb[:, i, 0:NT], start=(k == 0), stop=(k == 7))],
                       reads=[rsq[i], rcb], writes=[rPS[7]])
            P.task("act", [I("activation", out=sdb[:, 0:NT], in_=PSb[7][:, 0:NT], func=AF.Sqrt, bias=epsc[:, 0:1], scale=1.0)],
                   reads=[rPS[7], rc], writes=[rsd])
            P.task("dve", [I("reciprocal", out=Rb[:, 0:NT], in_=sdb[:, 0:NT])], reads=[rsd], writes=[rR])
            for k in range(8):
                P.task("dve", [I("scalar_tensor_tensor", out=out_ap_fn(k), in0=X[:, k, t0:t1], scalar=cv(gcol + k),
                                 in1=Rb[:, 0:NT], op0=ALU.mult, op1=ALU.mult)],
                       reads=[rR, rc] + [rX[b][k] for b in range(b0, b1)], writes=out_res_fn(k))

        epsc = sb("epsc", [128, 2], F32)
        P.task("dve", [I("memset", epsc[:, 0:1], EPS), I("memset", epsc[:, 1:2], 0.0)], writes=[rc])

        ctr = {"gu": 0, "dn": 0, "st": 0, "step": 0}

        def ffn(fid, gcol, fb0, next_fid=None):
            tl = split_tiles(fb0, NB)
            load_group(fid, 0)
            load_group(fid, 1)
            for (b0, b1) in tl:
                rmsnorm_tile(b0, b1, gcol,
                             lambda k, b0=b0, b1=b1: Hall[:, k, b0 * 128:b1 * 128],
                             lambda k, b0=b0, b1=b1: [rH[b][k] for b in range(b0, b1)])
            NG = len(GROUPS)

            def emit_gu(g, ti, slot, buf):
                b0, b1 = tl[ti]
                t0, t1 = b0 * 128, b1 * 128
                NT = t1 - t0
                G = GROUPS[g]
                base = slot * 12288
                wgv = WR[:, base:base + 4096].rearrange("p (k n) -> p k n", k=8)
                wuv = WR[:, base + 4096:base + 8192].rearrange("p (k n) -> p k n", k=8)
                hres = [rH[b][k] for b in range(b0, b1) for k in range(8)]
                for j in range(G):
                    gb = ctr["gu"] % 2
                    ub = 2 + ctr["gu"] % 2
                    ctr["gu"] += 1
                    P.task("pe", [I("matmul", PSb[gb][:, 0:NT], lhsT=wgv[:, k, j * 128:(j + 1) * 128], rhs=Hall[:, k, t0:t1],
                                    start=(k == 0), stop=(k == 7)) for k in range(8)],
                           reads=[rW[slot]] + hres, writes=[rPS[gb]])
                    P.task("pe", [I("matmul", PSb[ub][:, 0:NT], lhsT=wuv[:, k, j * 128:(j + 1) * 128], rhs=Hall[:, k, t0:t1],
                                    start=(k == 0), stop=(k == 7)) for k in range(8)],
                           reads=[rW[slot]] + hres, writes=[rPS[ub]])
                    si = ctr["st"] % 2
                    ctr["st"] += 1
                    P.task("act", [I("activation", out=stmp[si][:, 0:NT], in_=PSb[gb][:, 0:NT], func=AF.Silu)],
                           reads=[rPS[gb]], writes=[rst[si]])
                    P.task("dve", [I("tensor_tensor", out=aT[buf][:, j, 0:NT], in0=stmp[si][:, 0:NT], in1=PSb[ub][:, 0:NT], op=ALU.mult)],
                           reads=[rst[si], rPS[ub]], writes=[raT[buf][j]])

            def emit_down(g, ti, slot, buf):
                b0, b1 = tl[ti]
                t0, t1 = b0 * 128, b1 * 128
                NT = t1 - t0
                G = GROUPS[g]
                base = slot * 12288
                wdv = WR[:, base + 8192:base + 12288].rearrange("p (j n) -> p j n", j=4)
                for d in range(8):
                    bk = 4 + ctr["dn"] % 2
                    ctr["dn"] += 1
                    P.task("pe", [I("matmul", PSb[bk][:, 0:NT], lhsT=wdv[:, j, d * 128:(d + 1) * 128], rhs=aT[buf][:, j, 0:NT],
                                    start=(j == 0), stop=(j == G - 1)) for j in range(G)],
                           reads=[rW[slot]] + [raT[buf][j] for j in range(G)], writes=[rPS[bk]])
                    P.task("dve", [I("scalar_tensor_tensor", out=X[:, d, t0:t1], in0=PSb[bk][:, 0:NT], scalar=0.5,
                                     in1=X[:, d, t0:t1], op0=ALU.mult, op1=ALU.add)],
                           reads=[rPS[bk]], writes=[rX[b][d] for b in range(b0, b1)])

            pend = None
            for g in range(NG):
                slot = load_group(fid, g)
                for ti in range(len(tl)):
                    buf = ctr["step"] % 2
                    ctr["step"] += 1
                    emit_gu(g, ti, slot, buf)
                    if pend is not None:
                        emit_down(*pend)
                        pg, pti = pend[0], pend[1]
                        if pti == len(tl) - 1:
                            if pg + 2 < NG:
                                load_group(fid, pg + 2)
                            elif next_fid is not None:
                                load_group(next_fid, pg + 2 - NG)
                    pend = (g, ti, slot, buf)
            emit_down(*pend)
            if next_fid is not None:
                load_group(next_fid, 1)

        def mixer(l):
            zb0, ob0 = l, l + 1
            P.barrier()
            s_in = next_slot()
            s_out = next_slot()
            assert s_in == 0 and s_out == 1
            Win = WR[:, 0:12288].rearrange("p (k n) -> p k n", k=8)
            Wout = WR[:, 12288:12288 + 8192].rearrange("p (k n) -> p k n", k=8)
            P.dma("pool", [I("dma_start", out=Win[:, 0:4, :], in_=win_d[l].rearrange("(k p) n -> p k n", p=128)[:, 0:4, :]),
                           I("dma_start", out=Win[:, 4:8, :], in_=win_d[l].rearrange("(k p) n -> p k n", p=128)[:, 4:8, :])],
                  writes=[rW[0]], sem=wsem[0])
            P.dma("pool", [I("dma_start", out=Wout, in_=wout_d[l].rearrange("(k p) n -> p k n", p=128))],
                  writes=[rW[1]], sem=wsem[1])
            for gI in range(2):
                P.task("dve", [I("tensor_copy", out=SM[gI][:, :, 256:257],
                                 in_=cv(62 + l * 8 + gI * 4, 4).unsqueeze(2))],
                       reads=[rc], writes=[rSM[gI]])
            tl = split_tiles(zb0, NB)
            zc = 0
            for ti, (b0, b1) in enumerate(tl):
                t0, t1 = b0 * 128, b1 * 128
                NT = t1 - t0
                nbk = b1 - b0
                o0 = max(b0, ob0)
                lo0 = (o0 - b0) * 128
                NO = NT - lo0
                to0 = o0 * 128
                rmsnorm_tile(b0, b1, l * 24 + 8, lambda k: Hm[:, k, 0:NT], lambda k: [rHm])
                for c in range(9):
                    bk = zc % 2
                    zc += 1
                    P.task("pe", [I("matmul", PSb[bk][:, 0:NT], lhsT=Win[:, k, c * 128:(c + 1) * 128], rhs=Hm[:, k, 0:NT],
                                    start=(k == 0), stop=(k == 7)) for k in range(8)],
                           reads=[rW[0], rHm], writes=[rPS[bk]])
                    if c < 4:
                        P.task("act", [I("activation", out=QTt[:, c, 0:NT], in_=PSb[bk][:, 0:NT], func=AF.Copy, scale=0.125)],
                               reads=[rPS[bk]], writes=[rQT])
                    elif c == 4:
                        P.task("dve", [I("tensor_copy", out=KT[:, t0:t1], in_=PSb[bk][:, 0:NT])],
                               reads=[rPS[bk]], writes=[rKT[b] for b in range(b0, b1)])
                    elif c < 7:
                        P.task("act", [I("activation", out=GUTt[:, c - 5, 0:NT], in_=PSb[bk][:, 0:NT], func=AF.Gelu_apprx_tanh)],
                               reads=[rPS[bk]], writes=[rGUT])
                    else:
                        P.task("dve", [I("tensor_copy", out=PTt[:, c - 7, 16:16 + NT], in_=PSb[bk][:, 0:NT])],
                               reads=[rPS[bk]], writes=[rPTt])
                for bi in (range(nbk) if "tm" in parts else []):
                    b = b0 + bi
                    P.task("pe", [I("matmul", PSb[2][:, 0:384], lhsT=Hm[:, k, bi * 128:(bi + 1) * 128], rhs=Win[:, k, 1152:1536],
                                    start=(k == 0), stop=(k == 7)) for k in range(8)],
                           reads=[rW[0], rHm], writes=[rPS[2]])
                    P.task("dve", [I("tensor_copy", out=VTM[:, b, :], in_=PSb[2][:, 0:128])], reads=[rPS[2]], writes=[rVT[b]])
                    P.task("act", [I("activation", out=glb, in_=PSb[2][:, 128:384], func=AF.Gelu_apprx_tanh)],
                           reads=[rPS[2]], writes=[rgl])
                    P.task("dve", [I("tensor_tensor", out=gsq, in0=glb, in1=glb, op=ALU.mult)],
                           reads=[rgl], writes=[rgsq])
                    P.task("dve", [I("tensor_reduce", out=small[:, 0:1], in_=gsq, axis=AX.X, op=ALU.add)],
                           reads=[rgsq], writes=[rsmall])
                    P.task("act", [I("activation", out=small[:, 1:2], in_=small[:, 0:1], func=AF.Sqrt, bias=epsc[:, 0:1], scale=1.0 / 256)],
                           reads=[rsmall, rc], writes=[rsmall])
                    P.task("dve", [I("reciprocal", out=small[:, 2:3], in_=small[:, 1:2])], reads=[rsmall], writes=[rsmall])
                    P.task("dve", [I("scalar_tensor_tensor", out=GVNt[:, bi, :], in0=glb, scalar=small[:, 2:3],
                                     in1=vng[:, l * 256:(l + 1) * 256], op0=ALU.mult, op1=ALU.mult)],
                           reads=[rsmall, rgl, rc], writes=[rGVN[bi]])
                for b in (range(o0, b1) if "attn" in parts else []):
                    bi = b - b0
                    q0, q1 = bi * 128, (bi + 1) * 128
                    k0, k1 = (b - 1) * 128, (b + 1) * 128
                    mk = masks[:, 256:512] if b == OWNB else masks[:, 0:256]
                    for gI in range(2):
                        sc = []
                        for cc in range(2):
                            c = 2 * gI + cc
                            sc.append(I("matmul", PSb[3][:, cc * 256:(cc + 1) * 256], lhsT=QTt[0:64, c, q0:q1], rhs=KT[0:64, k0:k1],
                                        start=True, stop=True))
                            sc.append(I("matmul", PSb[4][:, cc * 256:(cc + 1) * 256], lhsT=QTt[64:128, c, q0:q1], rhs=KT[64:128, k0:k1],
                                        start=True, stop=True))
                        P.task("pe", sc, reads=[rQT, rKT[b - 1], rKT[b]], writes=[rPS[3], rPS[4]])
                        Sv = PS[:, 3:5, :].rearrange("p b (s k) -> p (b s) k", s=2)
                        P.task("dve", [I("tensor_tensor", out=SM[gI][:, :, 0:256], in0=Sv,
                                         in1=mk.unsqueeze(1).broadcast_to([128, 4, 256]), op=ALU.add)],
                               reads=[rPS[3], rPS[4], rc], writes=[rSM[gI]])
                        P.task("dve", [I("tensor_reduce", out=small[:, 8:12], in_=SM[gI][:, :, 0:257], axis=AX.X, op=ALU.max)],
                               reads=[rSM[gI]], writes=[rsmall])
                        P.task("dve", [I("tensor_tensor", out=Eb[:, :, 0:257], in0=SM[gI][:, :, 0:257],
                                         in1=small[:, 8:12].unsqueeze(2).broadcast_to([128, 4, 257]), op=ALU.subtract)],
                               reads=[rsmall, rSM[gI]], writes=[rE])
                        P.task("act", [I("activation", out=Eb[:, :, 0:257], in_=Eb[:, :, 0:257], func=AF.Exp)],
                               reads=[rE], writes=[rE])
                        P.task("dve", [I("tensor_reduce", out=small[:, 12:16], in_=Eb[:, :, 0:257], axis=AX.X, op=ALU.add)],
                               reads=[rE], writes=[rsmall])
                        P.task("dve", [I("reciprocal", out=small[:, 16:20], in_=small[:, 12:16])], reads=[rsmall], writes=[rsmall])
                        P.task("dve", [I("tensor_tensor", out=Pm[:, :, :], in0=Eb[:, :, 0:256],
                                         in1=small[:, 16:20].unsqueeze(2).broadcast_to([128, 4, 256]), op=ALU.mult)],
                               reads=[rsmall, rE], writes=[rPm])
                        PTp = PSb[5].bitcast(BF16).rearrange("p (s k q) -> p s k q", s=4, k=2)[:, :, :, :]
                        P.task("pe", [I("transpose", PTp[:, s, kb, :], Pm[:, s, kb * 128:(kb + 1) * 128], ident[:])
                                      for s in range(4) for kb in range(2)],
                               reads=[rPm, rcb], writes=[rPS[5]])
                        P.task("act", [I("activation", out=PTs[:, :, :, :], in_=PTp, func=AF.Copy)],
                               reads=[rPS[5]], writes=[rPTs])
                        pv = []
                        for s in range(4):
                            cc, ty = s % 2, s // 2
                            for kb in range(2):
                                pv.append(I("matmul", PSb[6][ty * 64:(ty + 1) * 64, cc * 128:(cc + 1) * 128],
                                            lhsT=VTM[:, b - 1 + kb, ty * 64:(ty + 1) * 64], rhs=PTs[:, s, kb, :],
                                            start=(kb == 0), stop=(kb == 1)))
                        P.task("pe", pv, reads=[rPTs, rVT[b - 1], rVT[b]], writes=[rPS[6]])
                        P.task("act", [I("activation", out=YTt[:, 2 * gI:2 * gI + 2, q0:q1],
                                         in_=PSb[6][:, 0:256].rearrange("p (c q) -> p c q", c=2), func=AF.Copy)],
                               reads=[rPS[6]], writes=[rYT[bi]])
                for b in (range(o0, b1) if "gmlp" in parts else []):
                    bi = b - b0
                    q0, q1 = bi * 128, (bi + 1) * 128
                    gm = []
                    for h in range(4):
                        pr, hh = h // 2, h % 2
                        gm.append(I("matmul", PSb[2][hh * 64:(hh + 1) * 64, pr * 128:(pr + 1) * 128],
                                    lhsT=GVNt[:, bi, h * 64:(h + 1) * 64], rhs=WT[:, (l * 4 + h) * 128:(l * 4 + h + 1) * 128],
                                    start=True, stop=True))
                    P.task("pe", gm, reads=[rGVN[bi], rcb], writes=[rPS[2]])
                    P.task("dve", [I("tensor_tensor", out=tgb, in0=PSb[2][:, 0:256], in1=biasB[:, l * 256:(l + 1) * 256], op=ALU.add)],
                           reads=[rPS[2], rc], writes=[rtg])
                    P.task("pool", [I("tensor_tensor", out=YTt[:, 4:6, q0:q1], in0=tgb.rearrange("p (c q) -> p c q", c=2),
                                      in1=GUTt[:, 0:2, q0:q1], op=ALU.mult)],
                           reads=[rtg, rGUT], writes=[rYT[bi]])
                U = 16 + NT
                ulo = 16 + lo0
                for c in (range(2) if "pool" in parts else []):
                    pc = PTt[:, c, :]
                    lo1, lo2, lo3 = ulo - 14, ulo - 12, ulo - 8
                    P.task("pool", [I("tensor_tensor", out=Sa[:, lo1:U], in0=pc[:, lo1:U], in1=pc[:, lo1 - 1:U - 1], op=ALU.add)],
                           reads=[rPTt], writes=[rSa])
                    P.task("pool", [I("tensor_tensor", out=Sb[:, lo2:U], in0=Sa[:, lo2:U], in1=Sa[:, lo2 - 2:U - 2], op=ALU.add)],
                           reads=[rSa], writes=[rSb])
                    if c == 0:
                        lowS, hiS, rlow, rhi = Sa, Sb, rSa, rSb
                    else:
                        P.task("pool", [I("tensor_tensor", out=Sa[:, lo3:U], in0=Sb[:, lo3:U], in1=Sb[:, lo3 - 4:U - 4], op=ALU.add)],
                               reads=[rSb], writes=[rSa])
                        P.task("pool", [I("tensor_tensor", out=Sb[:, ulo:U], in0=Sa[:, ulo:U], in1=Sa[:, ulo - 8:U - 8], op=ALU.add)],
                               reads=[rSa], writes=[rSb])
                        lowS, hiS, rlow, rhi = Sa, Sb, rSa, rSb
                    for (p0, p1, Sx, rS) in ((0, 64, lowS, rlow), (64, 128, hiS, rhi)):
                        P.task("dve", [I("scalar_tensor_tensor", out=dTt[p0:p1, c, ulo:U], in0=Sx[p0:p1, ulo:U],
                                         scalar=cvec[p0:p1, 60 + c:61 + c], in1=PTt[p0:p1, c, ulo:U],
                                         op0=ALU.mult, op1=ALU.subtract)],
                               reads=[rS, rPTt, rc], writes=[rdT])
                        if b0 <= OWNB < b1 and o0 <= OWNB:
                            u0 = 16 + (OWNB - b0) * 128
                            P.task("dve", [I("tensor_tensor", out=t16[p0:p1, :], in0=Sx[p0:p1, u0:u0 + 16],
                                             in1=poolc[p0:p1, c * 16:(c + 1) * 16], op=ALU.mult)],
                                   reads=[rS, rc], writes=[rt16])
                            P.task("dve", [I("tensor_tensor", out=dTt[p0:p1, c, u0:u0 + 16], in0=t16[p0:p1, :],
                                             in1=PTt[p0:p1, c, u0:u0 + 16], op=ALU.subtract)],
                                   reads=[rt16, rPTt], writes=[rdT])
                    P.task("pe", [I("matmul", PSb[7][:, 0:NO], lhsT=BD[:, (l * 2 + c) * 128:(l * 2 + c + 1) * 128], rhs=dTt[:, c, ulo:U],
                                    start=True, stop=True)],
                           reads=[rdT, rcb], writes=[rPS[7]])
                    P.task("dve", [I("tensor_scalar", out=YTt[:, 6 + c, lo0:NT], in0=PSb[7][:, 0:NO], scalar1=cv(56 + l * 2 + c), scalar2=None, op0=ALU.mult)],
                           reads=[rPS[7], rc], writes=rYT[0:nbk])
                if ti < len(tl) - 1:
                    P.task("pool", [I("tensor_copy", out=PTt[:, :, 0:16], in_=PTt[:, :, NT:NT + 16])],
                           reads=[rPTt], writes=[rPTt])
                for d in (range(8) if "wout" in parts else []):
                    bk = zc % 2
                    zc += 1
                    P.task("pe", [I("matmul", PSb[bk][:, 0:NO], lhsT=Wout[:, m, d * 128:(d + 1) * 128], rhs=YTt[:, m, lo0:NT],
                                    start=(m == 0), stop=(m == 7)) for m in range(8)],
                           reads=[rW[1]] + rYT[0:nbk], writes=[rPS[bk]])
                    P.task("dve", [I("tensor_tensor", out=X[:, d, to0:t1], in0=PSb[bk][:, 0:NO], in1=X[:, d, to0:t1], op=ALU.add)],
                           reads=[rPS[bk]], writes=[rX[b][d] for b in range(o0, b1)])
            P.barrier()

        phase = 0
        done = False
        for l in range(DEPTH):
            ffn(2 * l, l * 24 + 0, l)
            if stop_after == phase:
                done = True
                break
            phase += 1
            mixer(l)
            if stop_after == phase:
                done = True
                break
            phase += 1
            nxt = 2 * l + 2 if (l + 1 < DEPTH and stop_after is None) else None
            ffn(2 * l + 1, l * 24 + 16, l + 1, next_fid=nxt)
            if stop_after == phase:
                done = True
                break
            phase += 1
        finals = []
        outv = out_d.rearrange("(k p) t -> p k t", p=128)
        for (b0, b1) in split_tiles(OWNB, NB):
            t0, t1 = b0 * 128, b1 * 128
            if not done:
                rmsnorm_tile(b0, b1, 48, lambda k: X[:, k, t0:t1], lambda k: [rX[b][k] for b in range(b0, b1)])
            finals.append(P.dma("sp", [I("dma_start", out=outv[:, :, t0 - HALO:t1 - HALO], in_=X[:, :, t0:t1])],
                                reads=[rX[b][k] for b in range(b0, b1) for k in range(8)]))
        P.emit(final_waits=finals)
    return nc


def _prep_shared(inp):
    f = np.float32
    sh = {}
    sh["wg"] = np.ascontiguousarray(np.stack([inp["ffn1_w_gate"][0], inp["ffn2_w_gate"][0], inp["ffn1_w_gate"][1], inp["ffn2_w_gate"][1]]), dtype=f)
    sh["wu"] = np.ascontiguousarray(np.stack([inp["ffn1_w_up"][0], inp["ffn2_w_up"][0], inp["ffn1_w_up"][1], inp["ffn2_w_up"][1]]), dtype=f)
    sh["wd"] = np.ascontiguousarray(np.stack([inp["ffn1_w_down"][0], inp["ffn2_w_down"][0], inp["ffn1_w_down"][1], inp["ffn2_w_down"][1]]), dtype=f)
    w_in = np.asarray(inp["w_in"], dtype=f)
    qcols = np.concatenate([np.arange(h * 64, (h + 1) * 64) for h in HEAD_PERM])
    cols = np.concatenate([qcols, np.arange(512, 640), np.arange(768, 1024), np.arange(1280, 1536),
                           np.arange(640, 768), np.arange(1024, 1280)])
    sh["win"] = np.ascontiguousarray(w_in[:, :, cols])
    w_out = np.asarray(inp["w_out"], dtype=f)
    rows = np.concatenate([qcols, np.arange(512, 1024)])
    sh["wout"] = np.ascontiguousarray(w_out[:, rows, :])
    cvec = np.zeros((128, NCV), dtype=f)
    for l in range(DEPTH):
        for wi, nm in enumerate(["ffn1_norm", "mix_norm", "ffn2_norm"]):
            cvec[:, l * 24 + wi * 8:l * 24 + wi * 8 + 8] = np.asarray(inp[nm][l], dtype=f).reshape(8, 128).T
        cvec[:, 56 + l * 2:58 + l * 2] = np.asarray(inp["pool_scale"][l], dtype=f).reshape(2, 128).T
        sk = np.asarray(inp["attn_sinks"][l], dtype=f)
        for gI in range(2):
            hs = [2 * gI, 2 * gI + 1, 2 * gI + 4, 2 * gI + 5]
            cvec[:, 62 + l * 8 + gI * 4:62 + l * 8 + gI * 4 + 4] = sk[hs][None, :]
    cvec[:, 48:56] = np.asarray(inp["final_norm"], dtype=f).reshape(8, 128).T
    invw = np.zeros((128, 2), dtype=f)
    invw[:64, 0], invw[64:, 0], invw[:64, 1], invw[64:, 1] = 1 / 2, 1 / 4, 1 / 8, 1 / 16
    cvec[:, 60:62] = invw
    sh["cvec"] = cvec
    sh["vng"] = np.ascontiguousarray(np.broadcast_to(np.asarray(inp["gmlp_v_norm"], dtype=f).reshape(1, DEPTH * 256), (128, DEPTH * 256)))
    ws = np.asarray(inp["gmlp_w_s"], dtype=f)
    sh["wsT"] = np.ascontiguousarray(ws.transpose(3, 0, 1, 2).reshape(128, DEPTH * 4 * 128))
    jj = np.arange(128)[:, None]
    ii = np.arange(128)[None, :]
    sh["trilT"] = (jj <= ii).astype(f)
    gb = np.asarray(inp["gmlp_b"], dtype=f)
    bB = np.zeros((128, DEPTH, 2, 128), dtype=f)
    for l in range(DEPTH):
        for h in range(4):
            pr, hh = h // 2, h % 2
            bB[hh * 64:(hh + 1) * 64, l, pr, :] = gb[l, h][None, :]
    sh["biasB"] = bB.reshape(128, DEPTH * 256)
    pw = np.asarray(inp["pool_w"], dtype=f)
    bd = np.zeros((128, DEPTH, 2, 128), dtype=f)
    for l in range(DEPTH):
        for g in range(4):
            c, gg = g // 2, g % 2
            bd[gg * 64:(gg + 1) * 64, l, c, gg * 64:(gg + 1) * 64] = pw[l, g]
    sh["poolbd"] = bd.reshape(128, DEPTH * 256)
    sh["ident"] = np.eye(128, dtype=f)
    sh["onesm"] = np.full((128, 128), 1.0 / D, dtype=f)
    return sh


def _prep_core(x, c):
    f = np.float32
    b, part = c // 4, c % 4
    s0 = part * OWN
    own = x[b, s0:s0 + OWN]
    if part == 0:
        halo = np.zeros((HALO, D), dtype=f)
    else:
        halo = x[b, s0 - HALO:s0]
    xT = np.ascontiguousarray(np.concatenate([halo, own], axis=0).T)
    qi = np.arange(128)[:, None]
    kj = np.arange(256)[None, :]
    valid = (kj > qi) & (kj <= qi + 128)
    band = np.where(valid, 0.0, MASKV).astype(f)
    first = band.copy()
    if part == 0:
        first[:, :128] = MASKV
    masks = np.concatenate([band, first], axis=1)
    wins = np.array([2, 4, 8, 16], dtype=f)
    poolc = np.zeros((128, 2, 16), dtype=f)
    tt = np.arange(16, dtype=f)
    for g in range(4):
        cch, gg = g // 2, g % 2
        if part == 0:
            val = 1.0 / np.minimum(tt + 1, wins[g])
        else:
            val = np.full(16, 1.0 / wins[g], dtype=f)
        poolc[gg * 64:(gg + 1) * 64, cch, :] = val[None, :]
    return {"xT": xT, "masks": masks, "poolc": poolc.reshape(128, 32)}


_NC_CACHE = {}


def kernel(**inputs):
    inp = {k: np.asarray(v) for k, v in inputs.items()}
    x = np.asarray(inp["x"], dtype=np.float32)
    sh = _prep_shared(inp)
    in_maps = []
    for c in range(NCORES):
        m = dict(sh)
        m.update(_prep_core(x, c))
        in_maps.append(m)
    if "nc" not in _NC_CACHE:
        _NC_CACHE["nc"] = build_program()
    nc = _NC_CACHE["nc"]
    res = run_bass_kernel_spmd(nc, in_maps, core_ids=list(range(NCORES)))
    out = np.empty((BATCH, SEQ, D), dtype=np.float32)
    for c in range(NCORES):
        b, part = c // 4, c % 4
        out[b, part * OWN:(part + 1) * OWN, :] = np.asarray(res.results[c]["out"]).T
    return out
```

```python
import contextlib
import numpy as np
import concourse.bass as bass
import concourse.mybir as mybir
from concourse.bass_utils import run_bass_kernel_spmd

F32 = mybir.dt.float32
BF16 = mybir.dt.bfloat16
AF = mybir.ActivationFunctionType
ALU = mybir.AluOpType
AX = mybir.AxisListType

ENGS = ("pe", "act", "dve", "pool", "sp")

D = 1024
DFF = 2816
NFC = DFF // 128
DEPTH = 2
SEQ = 8192
BATCH = 2
NCORES = 8
OWN = 2048
HALO = 256
T = OWN + HALO
NB = T // 128
OWNB = HALO // 128
INW = 1536
EPS = 1e-6
GROUPS = [4, 4, 4, 4, 3, 3]
NCV = 78
HEAD_PERM = [0, 4, 1, 5, 2, 6, 3, 7]
MASKV = -30000.0


class Res:
    __slots__ = ("name", "writer", "readers")

    def __init__(self, name=""):
        self.name = name
        self.writer = None
        self.readers = []


class Prog:
    def __init__(self, nc):
        self.nc = nc
        self.ops = {e: [] for e in ENGS}
        self.count = {e: 0 for e in ENGS}
        self.waited = {e: {} for e in ENGS}
        self.dma_waited = {e: {} for e in ENGS}
        self.pending = {e: {} for e in ENGS}
        self.n_dma_sem = 0
        self.dma_count = {}

    def _collect(self, reads, writes):
        deps = []
        for r in reads:
            if r.writer is not None:
                deps.append(r.writer)
        for w in writes:
            if w.writer is not None:
                deps.append(w.writer)
            deps.extend(w.readers)
        return deps

    def _waits(self, eng, deps):
        waits = {}
        if self.pending[eng]:
            for de, idx in self.pending[eng].items():
                deps = deps + [(de, idx)]
            self.pending[eng] = {}
        for d in deps:
            if d[0] == "dma":
                _, si, val = d
                if self.dma_waited[eng].get(si, 0) >= val:
                    continue
                key = ("dma", si)
                waits[key] = max(waits.get(key, 0), val)
            else:
                de, idx = d
                if idx <= 0:
                    continue
                if de == eng and eng in ("pe", "sp"):
                    continue
                if self.waited[eng].get(de, 0) >= idx:
                    continue
                waits[de] = max(waits.get(de, 0), idx)
        out = []
        for k, v in waits.items():
            if isinstance(k, tuple):
                self.dma_waited[eng][k[1]] = v
            else:
                self.waited[eng][k] = v
            out.append((k, v))
        return out

    def barrier(self):
        snap = dict(self.count)
        for e in ENGS:
            for de, idx in snap.items():
                if de == "sp":
                    continue
                self.pending[e][de] = max(self.pending[e].get(de, 0), idx)

    def task(self, eng, instrs, reads=(), writes=()):
        psr = [r for r in reads if r.name.startswith("ps")]
        if psr:
            reads = [r for r in reads if not r.name.startswith("ps")]
            writes = list(writes) + psr
        deps = self._collect(reads, writes)
        waits = self._waits(eng, deps)
        self.count[eng] += 1
        me = (eng, self.count[eng])
        self.ops[eng].append((waits, list(instrs), None))
        for r in reads:
            r.readers.append(me)
        for w in writes:
            w.writer = me
            w.readers = []
        return me

    def new_dma_sem(self):
        i = self.n_dma_sem
        self.n_dma_sem += 1
        self.dma_count[i] = 0
        return i

    def dma(self, eng, instrs, reads=(), writes=(), sem=None):
        deps = self._collect(reads, writes)
        waits = self._waits(eng, deps)
        if sem is None:
            sem = self.new_dma_sem()
        self.dma_count[sem] += 16 * len(instrs)
        me = ("dma", sem, self.dma_count[sem])
        self.ops[eng].append((waits, list(instrs), sem))
        for r in reads:
            r.readers.append(me)
        for w in writes:
            w.writer = me
            w.readers = []
        return me

    def emit(self, final_waits=()):
        nc = self.nc
        with contextlib.ExitStack() as st:
            esem = {e: st.enter_context(nc.semaphore("s_" + e)) for e in ENGS if e != "sp"}
            dsem = [st.enter_context(nc.semaphore("d%d" % i)) for i in range(self.n_dma_sem)]
            block = st.enter_context(nc.Block())
            ops = self.ops

            def run(ename, e):
                for waits, instrs, dsi in ops[ename]:
                    for k, v in waits:
                        if isinstance(k, tuple):
                            e.wait_ge(dsem[k[1]], v)
                        else:
                            e.wait_ge(esem[k], v)
                    n = len(instrs)
                    for j, f in enumerate(instrs):
                        ins = f(e)
                        if dsi is not None:
                            ins.then_inc(dsem[dsi], 16)
                        elif j == n - 1:
                            ins.then_inc(esem[ename], 1)

            @block.tensor
            def _(e):
                run("pe", e)

            @block.scalar
            def _(e):
                run("act", e)

            @block.vector
            def _(e):
                run("dve", e)

            @block.gpsimd
            def _(e):
                run("pool", e)

            @block.sync
            def _(e):
                run("sp", e)
                for d in final_waits:
                    if d[0] == "dma":
                        e.wait_ge(dsem[d[1]], d[2])
                    else:
                        e.wait_ge(esem[d[0]], d[1])


def I(name, *args, **kw):
    return lambda e: getattr(e, name)(*args, **kw)


def split_tiles(b0, b1, mx=4):
    n = b1 - b0
    k = -(-n // mx)
    base, rem = divmod(n, k)
    out = []
    b = b0
    for i in range(k):
        s = base + (1 if i < rem else 0)
        out.append((b, b + s))
        b += s
    return out


def build_program(stop_after=None, parts=("z", "tm", "attn", "gmlp", "pool", "wout")):
    nc = bass.Bass("TRN2", target_bir_lowering=False)

    def din(name, shape):
        return nc.dram_tensor(name, list(shape), F32, kind="ExternalInput").ap()

    xT_d = din("xT", [D, T])
    wg_d = din("wg", [2 * DEPTH, D, DFF])
    wu_d = din("wu", [2 * DEPTH, D, DFF])
    wd_d = din("wd", [2 * DEPTH, DFF, D])
    win_d = din("win", [DEPTH, D, INW])
    wout_d = din("wout", [DEPTH, D, D])
    cvec_d = din("cvec", [128, NCV])
    poolc_d = din("poolc", [128, 32])
    vng_d = din("vng", [128, DEPTH * 256])
    wsT_d = din("wsT", [128, DEPTH * 4 * 128])
    tril_d = din("trilT", [128, 128])
    biasB_d = din("biasB", [128, DEPTH * 256])
    poolbd_d = din("poolbd", [128, DEPTH * 256])
    masks_d = din("masks", [128, 512])
    ident_d = din("ident", [128, 128])
    ones_d = din("onesm", [128, 128])
    out_d = nc.dram_tensor("out", [D, OWN], F32, kind="ExternalOutput").ap()

    with contextlib.ExitStack() as st:
        def sb(name, shape, dt):
            return st.enter_context(nc.sbuf_tensor("s_" + name, list(shape), dt))

        X = sb("X", [128, 8, T], F32)
        WR = sb("WR", [128, 24576], BF16)
        PHN = 18592
        PH = sb("PH", [128, PHN], F32)
        cvec = sb("cvec", [128, NCV], F32)
        poolc = sb("poolc", [128, 32], F32)
        vng = sb("vng", [128, DEPTH * 256], F32)
        WT = sb("WT", [128, DEPTH * 512], BF16)
        tril = sb("tril", [128, 128], BF16)
        biasB = sb("biasB", [128, DEPTH * 256], F32)
        BD = sb("BD", [128, DEPTH * 256], BF16)
        masks = sb("masks", [128, 512], F32)
        ident = sb("ident", [128, 128], BF16)
        onesm = sb("onesm", [128, 128], BF16)
        sqb = sb("sqb", [128, 2, 512], BF16)
        sdb = sb("sdb", [128, 512], F32)
        Rb = sdb
        small = sb("small", [128, 64], F32)
        PS = st.enter_context(nc.psum_tensor("PS", [128, 8, 512], F32))

        P = Prog(nc)
        rPS = [Res("ps%d" % i) for i in range(8)]
        rX = [[Res("x%d_%d" % (b, k)) for k in range(8)] for b in range(NB)]
        rW = [Res("w0"), Res("w1")]
        wsem = [P.new_dma_sem(), P.new_dma_sem()]
        wctr = [0]
        rsq = [Res("sq0"), Res("sq1")]
        rsd = Res("sd")
        rR = Res("R")
        rsmall = Res("small")

        def ph_f32(off, n):
            return PH[:, off:off + n]

        def ph_bf(off, n):
            return PH[:, off:off + n // 2].bitcast(BF16)

        Hall = ph_bf(0, 8 * T).rearrange("p (k t) -> p k t", k=8)
        aT = [ph_bf(9216 + i * 1024, 2048).rearrange("p (j t) -> p j t", j=4) for i in range(2)]
        stmp = [ph_f32(11264 + i * 512, 512) for i in range(2)]
        rH = [[Res("h%d_%d" % (b, k)) for k in range(8)] for b in range(NB)]
        raT = [[Res("aT%d_%d" % (i, j)) for j in range(4)] for i in range(2)]
        rst = [Res("st0"), Res("st1")]
        o = 0
        Hm = ph_bf(o, 4096).rearrange("p (k t) -> p k t", k=8); o += 2048
        QTt = ph_bf(o, 2048).rearrange("p (k t) -> p k t", k=4); o += 1024
        GUTt = ph_bf(o, 1024).rearrange("p (k t) -> p k t", k=2); o += 512
        GVNt = ph_bf(o, 1024).rearrange("p (b c) -> p b c", b=4); o += 512
        YTt = ph_bf(o, 4096).rearrange("p (k t) -> p k t", k=8); o += 2048
        dTt = ph_bf(o, 1056).rearrange("p (k t) -> p k t", k=2); o += 528
        Pm = [ph_bf(o + i * 512, 1024).rearrange("p (s k) -> p s k", s=4) for i in range(2)]; o += 1024
        PTs = [ph_bf(o + i * 512, 1024).rearrange("p (s k q) -> p s k q", s=4, k=2) for i in range(2)]; o += 1024
        KT = ph_bf(o, T); o += T // 2
        VTM = ph_bf(o, T).rearrange("p (b c) -> p b c", b=NB); o += T // 2
        PTt = ph_f32(o, 1056).rearrange("p (k t) -> p k t", k=2); o += 1056
        SM = [ph_f32(o + i * 1040, 1040).rearrange("p (s k) -> p s k", s=4) for i in range(2)]; o += 2080
        Eb = [ph_f32(o + i * 1040, 1040).rearrange("p (s k) -> p s k", s=4) for i in range(2)]; o += 2080
        Sa = ph_f32(o, 528); o += 528
        Sb = ph_f32(o, 528); o += 528
        glb = [ph_f32(o + i * 256, 256) for i in range(2)]; o += 512
        gsq = [ph_f32(o + i * 256, 256) for i in range(2)]; o += 512
        tgb = ph_f32(o, 256); o += 256
        t16 = ph_f32(o, 16); o += 16
        assert o <= PHN, o
        rHm, rQT, rGUT, rdT, rPTt = (Res(n) for n in "Hm QT GUT dT PTt".split())
        rPm = [Res("Pm0"), Res("Pm1")]
        rPTs = [Res("PTs0"), Res("PTs1")]
        rE = [Res("E0"), Res("E1")]
        rmx = [Res("mx0"), Res("mx1")]
        rden = [Res("den0"), Res("den1")]
        rrinv = [Res("rinv0"), Res("rinv1")]
        rtm = [[Res("tm%d_%d" % (i, j)) for j in range(3)] for i in range(2)]
        rGVN = [Res("gvn%d" % i) for i in range(4)]
        rYT = [Res("yt%d" % i) for i in range(4)]
        rKT = [Res("kt%d" % i) for i in range(NB)]
        rVT = [Res("vt%d" % i) for i in range(NB)]
        rSM = [Res("sm0"), Res("sm1")]
        rSa, rSb, rtg, rt16 = (Res(n) for n in "Sa Sb tg t16".split())
        rgl = [Res("gl0"), Res("gl1")]
        rgsq = [Res("gsq0"), Res("gsq1")]

        PSb = [PS[:, i, :] for i in range(8)]

        rc = Res("consts")
        cdeps = []
        cdeps.append(P.dma("sp", [
            I("dma_start", out=cvec[:], in_=cvec_d),
            I("dma_start", out=poolc[:], in_=poolc_d),
            I("dma_start", out=vng[:], in_=vng_d),
            I("dma_start", out=biasB[:], in_=biasB_d),
            I("dma_start", out=masks[:], in_=masks_d),
        ], writes=[rc]))
        rcb = Res("constsb")
        P.dma("pool", [
            I("dma_start", out=WT[:], in_=wsT_d),
            I("dma_start", out=tril[:], in_=tril_d),
            I("dma_start", out=BD[:], in_=poolbd_d),
            I("dma_start", out=ident[:], in_=ident_d),
            I("dma_start", out=onesm[:], in_=ones_d),
        ], writes=[rcb])
        xTv = xT_d.rearrange("(k p) t -> p k t", p=128)
        for (b0, b1) in split_tiles(0, NB):
            P.dma("sp", [I("dma_start", out=X[:, :, b0 * 128:b1 * 128], in_=xTv[:, :, b0 * 128:b1 * 128])],
                  writes=[rX[b][k] for b in range(b0, b1) for k in range(8)])
        P.task("dve", [I("tensor_tensor", out=WT[:].rearrange("p (a i) -> p a i", i=128),
                         in0=WT[:].rearrange("p (a i) -> p a i", i=128),
                         in1=tril[:].unsqueeze(1).broadcast_to([128, DEPTH * 4, 128]), op=ALU.mult)],
               reads=[rc], writes=[rcb])

        def cv(col, n=1):
            return cvec[:, col:col + n]

        def next_slot():
            s = wctr[0] % 2
            wctr[0] += 1
            return s

        loaded = {}

        def load_group(fid, g):
            if (fid, g) in loaded:
                return loaded[(fid, g)]
            s = next_slot()
            c0 = sum(GROUPS[:g])
            G = GROUPS[g]
            base = s * 12288
            wgv = WR[:, base:base + 4096].rearrange("p (k n) -> p k n", k=8)[:, :, 0:G * 128]
            wuv = WR[:, base + 4096:base + 8192].rearrange("p (k n) -> p k n", k=8)[:, :, 0:G * 128]
            wdv = WR[:, base + 8192:base + 12288].rearrange("p (j n) -> p j n", j=4)[:, 0:G, :]
            P.dma("pool", [
                I("dma_start", out=wgv, in_=wg_d[fid].rearrange("(k p) n -> p k n", p=128)[:, :, c0 * 128:(c0 + G) * 128]),
                I("dma_start", out=wuv, in_=wu_d[fid].rearrange("(k p) n -> p k n", p=128)[:, :, c0 * 128:(c0 + G) * 128]),
                I("dma_start", out=wdv, in_=wd_d[fid][c0 * 128:(c0 + G) * 128, :].rearrange("(j p) n -> p j n", p=128)),
            ], writes=[rW[s]], sem=wsem[s])
            loaded[(fid, g)] = s
            return s

        sqc = [0]

        def rmsnorm_tile(b0, b1, gcol, out_ap_fn, out_res_fn):
            t0, t1 = b0 * 128, b1 * 128
            NT = t1 - t0
            for k in range(8):
                i = sqc[0] % 2
                sqc[0] += 1
                P.task("pool", [I("tensor_tensor", out=sqb[:, i, 0:NT], in0=X[:, k, t0:t1], in1=X[:, k, t0:t1], op=ALU.mult)],
                       reads=[rX[b][k] for b in range(b0, b1)], writes=[rsq[i]])
                P.task("pe", [I("matmul", PSb[7][:, 0:NT], lhsT=onesm[:], rhs=sqb[:, i, 0:NT], start=(k == 0), stop=(k == 7))],
                       reads=[rsq[i], rcb], writes=[rPS[7]])
            P.task("act", [I("activation", out=sdb[:, 0:NT], in_=PSb[7][:, 0:NT], func=AF.Sqrt, bias=epsc[:, 0:1], scale=1.0)],
                   reads=[rPS[7], rc], writes=[rsd, rR])
            P.task("dve", [I("reciprocal", out=Rb[:, 0:NT], in_=sdb[:, 0:NT])], reads=[rsd], writes=[rR, rsd])
            for k in range(8):
                P.task("dve", [I("scalar_tensor_tensor", out=out_ap_fn(k), in0=X[:, k, t0:t1], scalar=cv(gcol + k),
                                 in1=Rb[:, 0:NT], op0=ALU.mult, op1=ALU.mult)],
                       reads=[rR, rc] + [rX[b][k] for b in range(b0, b1)], writes=out_res_fn(k))

        epsc = sb("epsc", [128, 2], F32)
        P.task("dve", [I("memset", epsc[:, 0:1], EPS), I("memset", epsc[:, 1:2], 0.0)], writes=[rc])

        ctr = {"gu": 0, "dn": 0, "st": 0, "step": 0}

        def ffn(fid, gcol, fb0, next_fid=None):
            tl = split_tiles(fb0, NB)
            load_group(fid, 0)
            load_group(fid, 1)
            for (b0, b1) in tl:
                rmsnorm_tile(b0, b1, gcol,
                             lambda k, b0=b0, b1=b1: Hall[:, k, b0 * 128:b1 * 128],
                             lambda k, b0=b0, b1=b1: [rH[b][k] for b in range(b0, b1)])
            NG = len(GROUPS)

            def emit_gu(g, ti, slot, buf):
                b0, b1 = tl[ti]
                t0, t1 = b0 * 128, b1 * 128
                NT = t1 - t0
                G = GROUPS[g]
                base = slot * 12288
                wgv = WR[:, base:base + 4096].rearrange("p (k n) -> p k n", k=8)
                wuv = WR[:, base + 4096:base + 8192].rearrange("p (k n) -> p k n", k=8)
                hres = [rH[b][k] for b in range(b0, b1) for k in range(8)]
                for j in range(G):
                    gb = ctr["gu"] % 2
                    ub = 2 + ctr["gu"] % 2
                    ctr["gu"] += 1
                    P.task("pe", [I("matmul", PSb[gb][:, 0:NT], lhsT=wgv[:, k, j * 128:(j + 1) * 128], rhs=Hall[:, k, t0:t1],
                                    start=(k == 0), stop=(k == 7)) for k in range(8)],
                           reads=[rW[slot]] + hres, writes=[rPS[gb]])
                    P.task("pe", [I("matmul", PSb[ub][:, 0:NT], lhsT=wuv[:, k, j * 128:(j + 1) * 128], rhs=Hall[:, k, t0:t1],
                                    start=(k == 0), stop=(k == 7)) for k in range(8)],
                           reads=[rW[slot]] + hres, writes=[rPS[ub]])
                    si = ctr["st"] % 2
                    ctr["st"] += 1
                    P.task("act", [I("activation", out=stmp[si][:, 0:NT], in_=PSb[gb][:, 0:NT], func=AF.Silu)],
                           reads=[rPS[gb]], writes=[rst[si]])
                    P.task("dve", [I("tensor_tensor", out=aT[buf][:, j, 0:NT], in0=stmp[si][:, 0:NT], in1=PSb[ub][:, 0:NT], op=ALU.mult)],
                           reads=[rst[si], rPS[ub]], writes=[raT[buf][j]])

            def emit_down(g, ti, slot, buf):
                b0, b1 = tl[ti]
                t0, t1 = b0 * 128, b1 * 128
                NT = t1 - t0
                G = GROUPS[g]
                base = slot * 12288
                wdv = WR[:, base + 8192:base + 12288].rearrange("p (j n) -> p j n", j=4)
                for d in range(8):
                    bk = 4 + ctr["dn"] % 2
                    ctr["dn"] += 1
                    P.task("pe", [I("matmul", PSb[bk][:, 0:NT], lhsT=wdv[:, j, d * 128:(d + 1) * 128], rhs=aT[buf][:, j, 0:NT],
                                    start=(j == 0), stop=(j == G - 1)) for j in range(G)],
                           reads=[rW[slot]] + [raT[buf][j] for j in range(G)], writes=[rPS[bk]])
                    P.task("dve", [I("scalar_tensor_tensor", out=X[:, d, t0:t1], in0=PSb[bk][:, 0:NT], scalar=0.5,
                                     in1=X[:, d, t0:t1], op0=ALU.mult, op1=ALU.add)],
                           reads=[rPS[bk]], writes=[rX[b][d] for b in range(b0, b1)])

            pend = None
            for g in range(NG):
                slot = load_group(fid, g)
                for ti in range(len(tl)):
                    buf = ctr["step"] % 2
                    ctr["step"] += 1
                    emit_gu(g, ti, slot, buf)
                    if pend is not None:
                        emit_down(*pend)
                        pg, pti = pend[0], pend[1]
                        if pti == len(tl) - 1:
                            if pg + 2 < NG:
                                load_group(fid, pg + 2)
                            elif next_fid is not None:
                                load_group(next_fid, pg + 2 - NG)
                    pend = (g, ti, slot, buf)
            emit_down(*pend)
            if next_fid is not None:
                load_group(next_fid, 1)

        def mixer(l):
            zb0, ob0 = l, l + 1
            P.barrier()
            s_in = next_slot()
            s_out = next_slot()
            assert s_in == 0 and s_out == 1
            Win = WR[:, 0:12288].rearrange("p (k n) -> p k n", k=8)
            Wout = WR[:, 12288:12288 + 8192].rearrange("p (k n) -> p k n", k=8)
            P.dma("pool", [I("dma_start", out=Win[:, 0:4, :], in_=win_d[l].rearrange("(k p) n -> p k n", p=128)[:, 0:4, :]),
                           I("dma_start", out=Win[:, 4:8, :], in_=win_d[l].rearrange("(k p) n -> p k n", p=128)[:, 4:8, :])],
                  writes=[rW[0]], sem=wsem[0])
            P.dma("pool", [I("dma_start", out=Wout, in_=wout_d[l].rearrange("(k p) n -> p k n", p=128))],
                  writes=[rW[1]], sem=wsem[1])
            for gI in range(2):
                P.task("dve", [I("tensor_copy", out=SM[gI][:, :, 256:257],
                                 in_=cv(62 + l * 8 + gI * 4, 4).unsqueeze(2))],
                       reads=[rc], writes=[rSM[gI]])
            tl = split_tiles(zb0, NB)
            zc = 0
            for ti, (b0, b1) in enumerate(tl):
                t0, t1 = b0 * 128, b1 * 128
                NT = t1 - t0
                nbk = b1 - b0
                o0 = max(b0, ob0)
                lo0 = (o0 - b0) * 128
                NO = NT - lo0
                to0 = o0 * 128
                rmsnorm_tile(b0, b1, l * 24 + 8, lambda k: Hm[:, k, 0:NT], lambda k: [rHm])
                for c in range(9):
                    bk = zc % 2
                    zc += 1
                    P.task("pe", [I("matmul", PSb[bk][:, 0:NT], lhsT=Win[:, k, c * 128:(c + 1) * 128], rhs=Hm[:, k, 0:NT],
                                    start=(k == 0), stop=(k == 7)) for k in range(8)],
                           reads=[rW[0], rHm], writes=[rPS[bk]])
                    if c < 4:
                        P.task("act", [I("activation", out=QTt[:, c, 0:NT], in_=PSb[bk][:, 0:NT], func=AF.Copy, scale=0.125)],
                               reads=[rPS[bk]], writes=[rQT])
                    elif c == 4:
                        P.task("dve", [I("tensor_copy", out=KT[:, t0:t1], in_=PSb[bk][:, 0:NT])],
                               reads=[rPS[bk]], writes=[rKT[b] for b in range(b0, b1)])
                    elif c < 7:
                        P.task("act", [I("activation", out=GUTt[:, c - 5, 0:NT], in_=PSb[bk][:, 0:NT], func=AF.Gelu_apprx_tanh)],
                               reads=[rPS[bk]], writes=[rGUT])
                    else:
                        P.task("dve", [I("tensor_copy", out=PTt[:, c - 7, 16:16 + NT], in_=PSb[bk][:, 0:NT])],
                               reads=[rPS[bk]], writes=[rPTt])
                for bi in (range(nbk) if "tm" in parts else []):
                    b = b0 + bi
                    P.task("pe", [I("matmul", PSb[2][:, 0:384], lhsT=Hm[:, k, bi * 128:(bi + 1) * 128], rhs=Win[:, k, 1152:1536],
                                    start=(k == 0), stop=(k == 7)) for k in range(8)],
                           reads=[rW[0], rHm], writes=[rPS[2]])
                    P.task("dve", [I("tensor_copy", out=VTM[:, b, :], in_=PSb[2][:, 0:128])], reads=[rPS[2]], writes=[rVT[b]])
                    tp = bi % 2
                    c0 = tp * 3
                    P.task("act", [I("activation", out=glb[tp], in_=PSb[2][:, 128:384], func=AF.Gelu_apprx_tanh)],
                           reads=[rPS[2]], writes=[rgl[tp]])
                    P.task("pool", [I("tensor_tensor", out=gsq[tp], in0=glb[tp], in1=glb[tp], op=ALU.mult)],
                           reads=[rgl[tp]], writes=[rgsq[tp]])
                    P.task("dve", [I("tensor_reduce", out=small[:, c0:c0 + 1], in_=gsq[tp], axis=AX.X, op=ALU.add)],
                           reads=[rgsq[tp]], writes=[rtm[tp][0]])
                    P.task("act", [I("activation", out=small[:, c0 + 1:c0 + 2], in_=small[:, c0:c0 + 1], func=AF.Sqrt, bias=epsc[:, 0:1], scale=1.0 / 256)],
                           reads=[rtm[tp][0], rc], writes=[rtm[tp][1]])
                    P.task("dve", [I("reciprocal", out=small[:, c0 + 2:c0 + 3], in_=small[:, c0 + 1:c0 + 2])], reads=[rtm[tp][1]], writes=[rtm[tp][2]])
                    P.task("dve", [I("scalar_tensor_tensor", out=GVNt[:, bi, :], in0=glb[tp], scalar=small[:, c0 + 2:c0 + 3],
                                     in1=vng[:, l * 256:(l + 1) * 256], op0=ALU.mult, op1=ALU.mult)],
                           reads=[rtm[tp][2], rgl[tp], rc], writes=[rGVN[bi]])
                def attn_stages(b, gI, par):
                    bi = b - b0
                    q0, q1 = bi * 128, (bi + 1) * 128
                    k0, k1 = (b - 1) * 128, (b + 1) * 128
                    mk = masks[:, 256:512] if b == OWNB else masks[:, 0:256]
                    SA, SB, PTb, Ob = (3, 4, 5, 6) if par == 0 else (0, 1, 2, 7)
                    mc = 8 + par * 12
                    negmx, den, rinv = small[:, mc:mc + 4], small[:, mc + 4:mc + 8], small[:, mc + 8:mc + 12]
                    Sv = PS[:, SA:SB + 1, :].rearrange("p b (s k) -> p (b s) k", s=2)
                    PTp = PSb[PTb].bitcast(BF16).rearrange("p (s k q) -> p s k q", s=4, k=2)

                    def st0():
                        sc = []
                        for cc in range(2):
                            c = 2 * gI + cc
                            sc.append(I("matmul", PSb[SA][:, cc * 256:(cc + 1) * 256], lhsT=QTt[0:64, c, q0:q1], rhs=KT[0:64, k0:k1],
                                        start=True, stop=True))
                            sc.append(I("matmul", PSb[SB][:, cc * 256:(cc + 1) * 256], lhsT=QTt[64:128, c, q0:q1], rhs=KT[64:128, k0:k1],
                                        start=True, stop=True))
                        P.task("pe", sc, reads=[rQT, rKT[b - 1], rKT[b]], writes=[rPS[SA], rPS[SB]])

                    def st1():
                        P.task("dve", [I("tensor_tensor", out=SM[gI][:, :, 0:256], in0=Sv,
                                         in1=mk.unsqueeze(1).broadcast_to([128, 4, 256]), op=ALU.add)],
                               reads=[rPS[SA], rPS[SB], rc], writes=[rSM[gI]])

                    def st2():
                        P.task("dve", [I("tensor_reduce", out=negmx, in_=SM[gI][:, :, 0:257], axis=AX.X, op=ALU.max, negate=True)],
                               reads=[rSM[gI]], writes=[rmx[par]])

                    def st3():
                        P.task("act", [I("activation", out=Eb[par][:, hs, 0:257], in_=SM[gI][:, hs, 0:257], func=AF.Exp,
                                         bias=negmx[:, hs:hs + 1], scale=1.0) for hs in range(4)],
                               reads=[rmx[par], rSM[gI]], writes=[rE[par]])

                    def st4():
                        P.task("dve", [I("tensor_reduce", out=den, in_=Eb[par][:, :, 0:257], axis=AX.X, op=ALU.add)],
                               reads=[rE[par]], writes=[rden[par]])
                        P.task("dve", [I("reciprocal", out=rinv, in_=den)], reads=[rden[par]], writes=[rrinv[par]])

                    def st5():
                        P.task("pool", [I("tensor_tensor", out=Pm[par][:, :, :], in0=Eb[par][:, :, 0:256],
                                          in1=rinv.unsqueeze(2).broadcast_to([128, 4, 256]), op=ALU.mult)],
                               reads=[rrinv[par], rE[par]], writes=[rPm[par]])

                    def st6():
                        P.task("pe", [I("transpose", PTp[:, hs, kb, :], Pm[par][:, hs, kb * 128:(kb + 1) * 128], ident[:])
                                      for hs in range(4) for kb in range(2)],
                               reads=[rPm[par], rcb], writes=[rPS[PTb]])

                    def st7():
                        P.task("act", [I("activation", out=PTs[par][:, :, :, :], in_=PTp, func=AF.Copy)],
                               reads=[rPS[PTb]], writes=[rPTs[par]])

                    def st8():
                        pv = []
                        for hs in range(4):
                            cc, ty = hs % 2, hs // 2
                            for kb in range(2):
                                pv.append(I("matmul", PSb[Ob][ty * 64:(ty + 1) * 64, cc * 128:(cc + 1) * 128],
                                            lhsT=VTM[:, b - 1 + kb, ty * 64:(ty + 1) * 64], rhs=PTs[par][:, hs, kb, :],
                                            start=(kb == 0), stop=(kb == 1)))
                        P.task("pe", pv, reads=[rPTs[par], rVT[b - 1], rVT[b]], writes=[rPS[Ob]])

                    def st9():
                        P.task("act", [I("activation", out=YTt[:, 2 * gI:2 * gI + 2, q0:q1],
                                         in_=PSb[Ob][:, 0:256].rearrange("p (c q) -> p c q", c=2), func=AF.Copy)],
                               reads=[rPS[Ob]], writes=[rYT[bi]])

                    return [st0, st1, st2, st3, st4, st5, st6, st7, st8, st9]

                agroups = [(b, gI) for b in (range(o0, b1) if "attn" in parts else []) for gI in range(2)]
                astages = [attn_stages(b, gI, gi % 2) for gi, (b, gI) in enumerate(agroups)]
                if astages:
                    NS = len(astages[0])
                    for step in range(len(agroups) + NS - 1):
                        for gi in range(len(agroups)):
                            sidx = step - gi
                            if 0 <= sidx < NS:
                                astages[gi][sidx]()
                for b in (range(o0, b1) if "gmlp" in parts else []):
                    bi = b - b0
                    q0, q1 = bi * 128, (bi + 1) * 128
                    gm = []
                    for h in range(4):
                        pr, hh = h // 2, h % 2
                        gm.append(I("matmul", PSb[2][hh * 64:(hh + 1) * 64, pr * 128:(pr + 1) * 128],
                                    lhsT=GVNt[:, bi, h * 64:(h + 1) * 64], rhs=WT[:, (l * 4 + h) * 128:(l * 4 + h + 1) * 128],
                                    start=True, stop=True))
                    P.task("pe", gm, reads=[rGVN[bi], rcb], writes=[rPS[2]])
                    P.task("dve", [I("tensor_tensor", out=tgb, in0=PSb[2][:, 0:256], in1=biasB[:, l * 256:(l + 1) * 256], op=ALU.add)],
                           reads=[rPS[2], rc], writes=[rtg])
                    P.task("pool", [I("tensor_tensor", out=YTt[:, 4:6, q0:q1], in0=tgb.rearrange("p (c q) -> p c q", c=2),
                                      in1=GUTt[:, 0:2, q0:q1], op=ALU.mult)],
                           reads=[rtg, rGUT], writes=[rYT[bi]])
                U = 16 + NT
                ulo = 16 + lo0
                for c in (range(2) if "pool" in parts else []):
                    pc = PTt[:, c, :]
                    lo1, lo2, lo3 = ulo - 14, ulo - 12, ulo - 8
                    P.task("pool", [I("tensor_tensor", out=Sa[:, lo1:U], in0=pc[:, lo1:U], in1=pc[:, lo1 - 1:U - 1], op=ALU.add)],
                           reads=[rPTt], writes=[rSa])
                    P.task("pool", [I("tensor_tensor", out=Sb[:, lo2:U], in0=Sa[:, lo2:U], in1=Sa[:, lo2 - 2:U - 2], op=ALU.add)],
                           reads=[rSa], writes=[rSb])
                    if c == 0:
                        lowS, hiS, rlow, rhi = Sa, Sb, rSa, rSb
                    else:
                        P.task("pool", [I("tensor_tensor", out=Sa[:, lo3:U], in0=Sb[:, lo3:U], in1=Sb[:, lo3 - 4:U - 4], op=ALU.add)],
                               reads=[rSb], writes=[rSa])
                        P.task("pool", [I("tensor_tensor", out=Sb[:, ulo:U], in0=Sa[:, ulo:U], in1=Sa[:, ulo - 8:U - 8], op=ALU.add)],
                               reads=[rSa], writes=[rSb])
                        lowS, hiS, rlow, rhi = Sa, Sb, rSa, rSb
                    for (p0, p1, Sx, rS) in ((0, 64, lowS, rlow), (64, 128, hiS, rhi)):
                        P.task("dve", [I("scalar_tensor_tensor", out=dTt[p0:p1, c, ulo:U], in0=Sx[p0:p1, ulo:U],
                                         scalar=cvec[p0:p1, 60 + c:61 + c], in1=PTt[p0:p1, c, ulo:U],
                                         op0=ALU.mult, op1=ALU.subtract)],
                               reads=[rS, rPTt, rc], writes=[rdT])
                        if b0 <= OWNB < b1 and o0 <= OWNB:
                            u0 = 16 + (OWNB - b0) * 128
                            P.task("dve", [I("tensor_tensor", out=t16[p0:p1, :], in0=Sx[p0:p1, u0:u0 + 16],
                                             in1=poolc[p0:p1, c * 16:(c + 1) * 16], op=ALU.mult)],
                                   reads=[rS, rc], writes=[rt16])
                            P.task("dve", [I("tensor_tensor", out=dTt[p0:p1, c, u0:u0 + 16], in0=t16[p0:p1, :],
                                             in1=PTt[p0:p1, c, u0:u0 + 16], op=ALU.subtract)],
                                   reads=[rt16, rPTt], writes=[rdT])
                    P.task("pe", [I("matmul", PSb[7][:, 0:NO], lhsT=BD[:, (l * 2 + c) * 128:(l * 2 + c + 1) * 128], rhs=dTt[:, c, ulo:U],
                                    start=True, stop=True)],
                           reads=[rdT, rcb], writes=[rPS[7]])
                    P.task("dve", [I("tensor_scalar", out=YTt[:, 6 + c, lo0:NT], in0=PSb[7][:, 0:NO], scalar1=cv(56 + l * 2 + c), scalar2=None, op0=ALU.mult)],
                           reads=[rPS[7], rc], writes=rYT[0:nbk])
                if ti < len(tl) - 1:
                    P.task("pool", [I("tensor_copy", out=PTt[:, :, 0:16], in_=PTt[:, :, NT:NT + 16])],
                           reads=[rPTt], writes=[rPTt])
                for d in (range(8) if "wout" in parts else []):
                    bk = zc % 2
                    zc += 1
                    P.task("pe", [I("matmul", PSb[bk][:, 0:NO], lhsT=Wout[:, m, d * 128:(d + 1) * 128], rhs=YTt[:, m, lo0:NT],
                                    start=(m == 0), stop=(m == 7)) for m in range(8)],
                           reads=[rW[1]] + rYT[0:nbk], writes=[rPS[bk]])
                    P.task("dve", [I("tensor_tensor", out=X[:, d, to0:t1], in0=PSb[bk][:, 0:NO], in1=X[:, d, to0:t1], op=ALU.add)],
                           reads=[rPS[bk]], writes=[rX[b][d] for b in range(o0, b1)])
            P.barrier()

        phase = 0
        done = False
        for l in range(DEPTH):
            ffn(2 * l, l * 24 + 0, l)
            if stop_after == phase:
                done = True
                break
            phase += 1
            mixer(l)
            if stop_after == phase:
                done = True
                break
            phase += 1
            nxt = 2 * l + 2 if (l + 1 < DEPTH and stop_after is None) else None
            ffn(2 * l + 1, l * 24 + 16, l + 1, next_fid=nxt)
            if stop_after == phase:
                done = True
                break
            phase += 1
        finals = []
        outv = out_d.rearrange("(k p) t -> p k t", p=128)
        for (b0, b1) in split_tiles(OWNB, NB):
            t0, t1 = b0 * 128, b1 * 128
            if not done:
                rmsnorm_tile(b0, b1, 48, lambda k: X[:, k, t0:t1], lambda k: [rX[b][k] for b in range(b0, b1)])
            finals.append(P.dma("sp", [I("dma_start", out=outv[:, :, t0 - HALO:t1 - HALO], in_=X[:, :, t0:t1])],
                                reads=[rX[b][k] for b in range(b0, b1) for k in range(8)]))
        P.emit(final_waits=finals)
    return nc


def _prep_shared(inp):
    f = np.float32
    sh = {}
    sh["wg"] = np.ascontiguousarray(np.stack([inp["ffn1_w_gate"][0], inp["ffn2_w_gate"][0], inp["ffn1_w_gate"][1], inp["ffn2_w_gate"][1]]), dtype=f)
    sh["wu"] = np.ascontiguousarray(np.stack([inp["ffn1_w_up"][0], inp["ffn2_w_up"][0], inp["ffn1_w_up"][1], inp["ffn2_w_up"][1]]), dtype=f)
    sh["wd"] = np.ascontiguousarray(np.stack([inp["ffn1_w_down"][0], inp["ffn2_w_down"][0], inp["ffn1_w_down"][1], inp["ffn2_w_down"][1]]), dtype=f)
    w_in = np.asarray(inp["w_in"], dtype=f)
    qcols = np.concatenate([np.arange(h * 64, (h + 1) * 64) for h in HEAD_PERM])
    cols = np.concatenate([qcols, np.arange(512, 640), np.arange(768, 1024), np.arange(1280, 1536),
                           np.arange(640, 768), np.arange(1024, 1280)])
    sh["win"] = np.ascontiguousarray(w_in[:, :, cols])
    w_out = np.asarray(inp["w_out"], dtype=f)
    rows = np.concatenate([qcols, np.arange(512, 1024)])
    sh["wout"] = np.ascontiguousarray(w_out[:, rows, :])
    cvec = np.zeros((128, NCV), dtype=f)
    for l in range(DEPTH):
        for wi, nm in enumerate(["ffn1_norm", "mix_norm", "ffn2_norm"]):
            cvec[:, l * 24 + wi * 8:l * 24 + wi * 8 + 8] = np.asarray(inp[nm][l], dtype=f).reshape(8, 128).T
        cvec[:, 56 + l * 2:58 + l * 2] = np.asarray(inp["pool_scale"][l], dtype=f).reshape(2, 128).T
        sk = np.asarray(inp["attn_sinks"][l], dtype=f)
        for gI in range(2):
            hs = [2 * gI, 2 * gI + 1, 2 * gI + 4, 2 * gI + 5]
            cvec[:, 62 + l * 8 + gI * 4:62 + l * 8 + gI * 4 + 4] = sk[hs][None, :]
    cvec[:, 48:56] = np.asarray(inp["final_norm"], dtype=f).reshape(8, 128).T
    invw = np.zeros((128, 2), dtype=f)
    invw[:64, 0], invw[64:, 0], invw[:64, 1], invw[64:, 1] = 1 / 2, 1 / 4, 1 / 8, 1 / 16
    cvec[:, 60:62] = invw
    sh["cvec"] = cvec
    sh["vng"] = np.ascontiguousarray(np.broadcast_to(np.asarray(inp["gmlp_v_norm"], dtype=f).reshape(1, DEPTH * 256), (128, DEPTH * 256)))
    ws = np.asarray(inp["gmlp_w_s"], dtype=f)
    sh["wsT"] = np.ascontiguousarray(ws.transpose(3, 0, 1, 2).reshape(128, DEPTH * 4 * 128))
    jj = np.arange(128)[:, None]
    ii = np.arange(128)[None, :]
    sh["trilT"] = (jj <= ii).astype(f)
    gb = np.asarray(inp["gmlp_b"], dtype=f)
    bB = np.zeros((128, DEPTH, 2, 128), dtype=f)
    for l in range(DEPTH):
        for h in range(4):
            pr, hh = h // 2, h % 2
            bB[hh * 64:(hh + 1) * 64, l, pr, :] = gb[l, h][None, :]
    sh["biasB"] = bB.reshape(128, DEPTH * 256)
    pw = np.asarray(inp["pool_w"], dtype=f)
    bd = np.zeros((128, DEPTH, 2, 128), dtype=f)
    for l in range(DEPTH):
        for g in range(4):
            c, gg = g // 2, g % 2
            bd[gg * 64:(gg + 1) * 64, l, c, gg * 64:(gg + 1) * 64] = pw[l, g]
    sh["poolbd"] = bd.reshape(128, DEPTH * 256)
    sh["ident"] = np.eye(128, dtype=f)
    sh["onesm"] = np.full((128, 128), 1.0 / D, dtype=f)
    return sh


def _prep_core(x, c):
    f = np.float32
    b, part = c // 4, c % 4
    s0 = part * OWN
    own = x[b, s0:s0 + OWN]
    if part == 0:
        halo = np.zeros((HALO, D), dtype=f)
    else:
        halo = x[b, s0 - HALO:s0]
    xT = np.ascontiguousarray(np.concatenate([halo, own], axis=0).T)
    qi = np.arange(128)[:, None]
    kj = np.arange(256)[None, :]
    valid = (kj > qi) & (kj <= qi + 128)
    band = np.where(valid, 0.0, MASKV).astype(f)
    first = band.copy()
    if part == 0:
        first[:, :128] = MASKV
    masks = np.concatenate([band, first], axis=1)
    wins = np.array([2, 4, 8, 16], dtype=f)
    poolc = np.zeros((128, 2, 16), dtype=f)
    tt = np.arange(16, dtype=f)
    for g in range(4):
        cch, gg = g // 2, g % 2
        if part == 0:
            val = 1.0 / np.minimum(tt + 1, wins[g])
        else:
            val = np.full(16, 1.0 / wins[g], dtype=f)
        poolc[gg * 64:(gg + 1) * 64, cch, :] = val[None, :]
    return {"xT": xT, "masks": masks, "poolc": poolc.reshape(128, 32)}


_NC_CACHE = {}


def kernel(**inputs):
    inp = {k: np.asarray(v) for k, v in inputs.items()}
    x = np.asarray(inp["x"], dtype=np.float32)
    sh = _prep_shared(inp)
    in_maps = []
    for c in range(NCORES):
        m = dict(sh)
        m.update(_prep_core(x, c))
        in_maps.append(m)
    if "nc" not in _NC_CACHE:
        _NC_CACHE["nc"] = build_program()
    nc = _NC_CACHE["nc"]
    res = run_bass_kernel_spmd(nc, in_maps, core_ids=list(range(NCORES)))
    out = np.empty((BATCH, SEQ, D), dtype=np.float32)
    for c in range(NCORES):
        b, part = c // 4, c % 4
        out[b, part * OWN:(part + 1) * OWN, :] = np.asarray(res.results[c]["out"]).T
    return out
```
